# Optimizing a Trainium2 kernel written in Bass

```python
import jax, jax.numpy as jnp
from jax import lax
import numpy as np

D_MODEL = 1024
BATCH = 8
SEQ = 4096
DEPTH = 1

D_MIX = D_MODEL
C_CONV = D_MIX // 2
C_POOL = D_MIX - C_CONV
CONV_HEADS = 8
CONV_WIDTH = 31
POOL_WINDOWS = (2, 4, 8, 16)
N_POOL_GROUPS = len(POOL_WINDOWS)
POOL_GROUP = C_POOL // N_POOL_GROUPS
D_IN = 2 * C_CONV + C_POOL
D_FF = ((8 * D_MODEL // 3 + 255) // 256) * 256
RMS_EPS = 1e-6
LN_EPS = 1e-5

kernel_name = "hybrid_conformer_conv_multiscale_pool_block"


def rmsnorm(x, g):
    xf = x.astype(jnp.float32)
    y = xf * lax.rsqrt(jnp.mean(xf * xf, axis=-1, keepdims=True) + RMS_EPS)
    return (y * g.astype(jnp.float32)).astype(x.dtype)


def layernorm(x, g, b):
    xf = x.astype(jnp.float32)
    mu = jnp.mean(xf, axis=-1, keepdims=True)
    var = jnp.mean(jnp.square(xf - mu), axis=-1, keepdims=True)
    y = (xf - mu) * lax.rsqrt(var + LN_EPS)
    return (y * g.astype(jnp.float32) + b.astype(jnp.float32)).astype(x.dtype)


def conformer_conv_group(a, gate, w_dw, b_dw, ln_g, ln_b):
    u = a * jax.nn.sigmoid(gate)
    k = w_dw.astype(u.dtype)[:, None, :]
    v = lax.conv_general_dilated(
        u, k, window_strides=(1,), padding=[(CONV_WIDTH - 1, 0)],
        dimension_numbers=("NWC", "WIO", "NWC"),
        feature_group_count=C_CONV) + b_dw.astype(u.dtype)
    v = layernorm(v, ln_g, ln_b)
    return jax.nn.silu(v)


def multiscale_pool_group(p, w_pool, s_pool):
    seq = p.shape[1]
    pos = jnp.arange(seq)
    outs = []
    for i, w in enumerate(POOL_WINDOWS):
        seg = p[..., i * POOL_GROUP:(i + 1) * POOL_GROUP].astype(jnp.float32)
        cs = jnp.cumsum(seg, axis=1)
        lag = jnp.pad(cs, ((0, 0), (w, 0), (0, 0)))[:, :seq]
        cnt = jnp.minimum(pos + 1, w).astype(jnp.float32)[None, :, None]
        y = ((cs - lag) / cnt - seg).astype(p.dtype)
        outs.append(jnp.einsum("bsg,gh->bsh", y, w_pool[i]))
    return jnp.concatenate(outs, axis=-1) * s_pool


def swiglu(x, w_gate, w_up, w_down):
    return (jax.nn.silu(x @ w_gate) * (x @ w_up)) @ w_down


def setup_inputs(seed: int = 0) -> dict:
    key = jax.random.key(seed)
    ks = jax.random.split(key, 20)
    f = jnp.float32
    n = lambda k, s, sc: jax.random.normal(k, s, f) * sc
    return {
        "x": jax.random.normal(ks[0], (BATCH, SEQ, D_MODEL), f),
        "g_mix": 1.0 + n(ks[1], (DEPTH, D_MODEL), 0.05),
        "w_in": n(ks[2], (DEPTH, D_MODEL, D_IN), D_MODEL ** -0.5),
        "b_in": n(ks[3], (DEPTH, D_IN), 0.02),
        "w_dw": n(ks[4], (DEPTH, CONV_WIDTH, C_CONV), CONV_WIDTH ** -0.5),
        "b_dw": n(ks[5], (DEPTH, C_CONV), 0.02),
        "ln_g": 1.0 + n(ks[6], (DEPTH, C_CONV), 0.05),
        "ln_b": n(ks[7], (DEPTH, C_CONV), 0.02),
        "w_pool": n(ks[8], (DEPTH, N_POOL_GROUPS, POOL_GROUP, POOL_GROUP), POOL_GROUP ** -0.5),
        "s_pool": 1.0 + n(ks[9], (DEPTH, C_POOL), 0.1),
        "w_out": n(ks[10], (DEPTH, D_MIX, D_MODEL), D_MIX ** -0.5),
        "g_ffn": 1.0 + n(ks[11], (DEPTH, D_MODEL), 0.05),
        "w_gate": n(ks[12], (DEPTH, D_MODEL, D_FF), D_MODEL ** -0.5),
        "w_up": n(ks[13], (DEPTH, D_MODEL, D_FF), D_MODEL ** -0.5),
        "w_down": n(ks[14], (DEPTH, D_FF, D_MODEL), D_FF ** -0.5),
        "g_final": 1.0 + n(ks[15], (D_MODEL,), 0.05),
    }


def reference(x, g_mix, w_in, b_in, w_dw, b_dw, ln_g, ln_b, w_pool, s_pool,
              w_out, g_ffn, w_gate, w_up, w_down, g_final):
    h = x
    for l in range(DEPTH):
        xn = rmsnorm(h, g_mix[l])
        z = xn @ w_in[l] + b_in[l]
        a = z[..., :C_CONV]
        gate = z[..., C_CONV:2 * C_CONV]
        p = z[..., 2 * C_CONV:]
        y_conv = conformer_conv_group(a, gate, w_dw[l], b_dw[l], ln_g[l], ln_b[l])
        y_pool = multiscale_pool_group(p, w_pool[l], s_pool[l])
        y = jnp.concatenate([y_conv, y_pool], axis=-1)
        h = h + y @ w_out[l]
        h = h + swiglu(rmsnorm(h, g_ffn[l]), w_gate[l], w_up[l], w_down[l])
    return rmsnorm(h, g_final)
```

```python
import numpy as np
import concourse.bass as bass
import concourse.mybir as mybir
from concourse.bass_utils import run_bass_kernel_spmd

F32 = mybir.dt.float32
BF16 = mybir.dt.bfloat16
ALU = mybir.AluOpType
AF = mybir.ActivationFunctionType

NCORES = 8
S = 4096
D = 1024
TB = 512
NTILE = S // TB
KD = D // 128
CW = 31
DIN = 1536
DFF = 2816
NF = DFF // 128
RMS_EPS = 1e-6
LN_EPS = 1e-5
NH = 8
NR = 3
ND = 4
NZ = 3

C_BIN = 0
C_GMIX = 12
C_GFFN = 20
C_BDW = 28
C_LNG = 32
C_LNB = 36
C_SPOOL = 40
C_WDW = 44
NV = 44 + 128


class Buf:
    __slots__ = ("name", "w", "r")

    def __init__(self, name):
        self.name = name
        self.w = None
        self.r = {}


class DSem:
    def __init__(self, handle):
        self.handle = handle
        self.count = 0


class Sched:
    ENGS = ("pe", "act", "dve", "pool", "sp")

    def __init__(self, sems):
        self.sem = sems
        self.tick = {e: 0 for e in self.ENGS}
        self.waited = {e: {} for e in self.ENGS}
        self.prog = {e: [] for e in self.ENGS}
        self.log = {e: [] for e in self.ENGS}
        self.needed = {e: set() for e in self.ENGS}

    def _wait(self, e, dep):
        kind, key, val = dep
        if kind == "e":
            if key == e and e in ("pe", "sp"):
                return
            k = ("e", key)
            semh = self.sem[key]
        else:
            k = ("d", id(key))
            semh = key.handle
            val = key.count
        w = self.waited[e]
        if w.get(k, 0) >= val:
            return
        w[k] = val
        self.log[e].append(("wait", k, val))
        if kind == "e":
            self.needed[key].add(val)
            self.prog[e].append(("ewait", key, val))
        else:
            self.prog[e].append(("dwait", semh, val))

    def _deps(self, e, reads, writes):
        for b in reads:
            if b.w is not None:
                self._wait(e, b.w)
        for b in writes:
            if b.w is not None:
                self._wait(e, b.w)
            for d in list(b.r.values()):
                self._wait(e, d)

    def op(self, e, fn, reads=(), writes=()):
        self._deps(e, reads, writes)
        self.tick[e] += 1
        me = ("e", e, self.tick[e])
        semh = self.sem[e]
        self.log[e].append(("inc", ("e", e), 1))
        self.prog[e].append(("op", fn, self.tick[e]))
        for b in reads:
            b.r[("e", e)] = me
        for b in writes:
            b.w = me
            b.r = {}

    def dma(self, q, out, in_, dsem, reads=(), writes=()):
        self._deps(q, reads, writes)
        dsem.count += 16
        me = ("d", dsem, dsem.count)
        self.log[q].append(("inc", ("d", id(dsem)), 16))
        self.prog[q].append(("dma", out, in_, dsem.handle))
        for b in reads:
            b.r[("d", id(dsem))] = me
        for b in writes:
            b.w = me
            b.r = {}

    def emit(self, e, h):
        import bisect
        rank = {f: sorted(self.needed[f]) for f in self.ENGS}
        for item in self.prog[e]:
            kind = item[0]
            if kind == "op":
                _, fn, tick = item
                ins = fn(h)
                if tick in self.needed[e]:
                    ins.then_inc(self.sem[e], 1)
            elif kind == "ewait":
                _, f, val = item
                h.wait_ge(self.sem[f], bisect.bisect_right(rank[f], val))
            elif kind == "dwait":
                _, semh, val = item
                h.wait_ge(semh, val)
            else:
                _, o, i, semh = item
                h.dma_start(out=o, in_=i).then_inc(semh, 16)

    def wait_dma(self, e, dsem):
        self._wait(e, ("d", dsem, dsem.count))


ZORDER = [4, 0, 5, 1, 6, 2, 7, 3, 8, 9, 10, 11]


def build_nc():
    nc = bass.Bass("TRN2", target_bir_lowering=False)

    def din(name, shape):
        return nc.dram_tensor(name, list(shape), F32, kind="ExternalInput").ap()

    x_d = din("x", [S, D])
    w_in_d = din("w_in", [D, DIN])
    w_out_d = din("w_out", [D, D])
    w_pool_d = din("w_pool", [4, 128, 128])
    w_gate_d = din("w_gate", [D, DFF])
    w_up_d = din("w_up", [D, DFF])
    w_down_d = din("w_down", [DFF, D])
    vecs_d = din("vecs", [128, NV])
    gfin_d = din("gfin", [128, D])
    ident_d = din("ident", [128, 128])
    invc_d = din("invc", [128, 16])
    stack_d = din("stack32", [128, 32])
    y_d = nc.dram_tensor("y", [S, D], F32, kind="ExternalOutput").ap()
    win_bf = nc.dram_tensor("win_bf", [D, DIN], BF16, kind="Internal").ap()
    wg_bf = nc.dram_tensor("wg_bf", [D, DFF], BF16, kind="Internal").ap()
    wu_bf = nc.dram_tensor("wu_bf", [D, DFF], BF16, kind="Internal").ap()
    wd_bf = nc.dram_tensor("wd_bf", [DFF, D], BF16, kind="Internal").ap()

    from contextlib import ExitStack
    with ExitStack() as es:
        def sb(name, shape, dt):
            return es.enter_context(nc.sbuf_tensor(name, list(shape), dt))

        w_out_sb = sb("w_out_sb", [128, KD, D], BF16)
        w_pool_sb = sb("w_pool_sb", [128, 4, 128], BF16)
        wp = sb("wp", [128, 4, 8, 4, 32], BF16)
        stack32 = sb("stack32_sb", [128, 32], BF16)
        u4 = [sb(f"u4_{q}", [128, 4, 528], BF16) for q in range(4)]
        ident = sb("ident_sb", [128, 128], BF16)
        ones = sb("ones_sb", [128, 128], BF16)
        vecs = sb("vecs_sb", [128, NV], F32)
        gfin = sb("gfin_sb", [128, D], F32)
        invc = sb("invc_sb", [128, 16], F32)
        epsr = sb("epsr", [128, 1], F32)
        epsl = sb("epsl", [128, 1], F32)
        hsl = [sb(f"hs{i}", [128, D], F32) for i in range(NH)]
        xs = [sb(f"xs{i}", [128, D], BF16) for i in range(2)]
        xnT = sb("xnT", [128, KD, TB], BF16)
        hnT = sb("hnT", [128, KD, TB], BF16)
        sig_t = [sb(f"sig{i}", [128, 2 * TB], BF16) for i in range(2)]
        sig = [sig_t[i][:].bitcast(F32) for i in range(2)]
        sgf = [sb(f"sgf{i}", [128, TB], F32) for i in range(2)]
        u = sb("u", [128, 4, 544], BF16)
        pb = sb("pb", [128, 4, 528], F32)
        ptmp = [sb(f"ptmp{i}", [128, 528], F32) for i in range(2)]
        ypool = sb("ypool", [128, 4, TB], BF16)
        v = sb("v", [128, 4, TB], F32)
        vbf = [sb(f"vbf{i}", [128, TB], BF16) for i in range(2)]
        sqb = [sb(f"sqb{i}", [128, TB], BF16) for i in range(2)]
        lnA_t = sb("lnA", [128, 2 * TB], BF16)
        lnB_t = sb("lnB", [128, 2 * TB], BF16)
        lnA = lnA_t[:].bitcast(F32)
        lnB = lnB_t[:].bitcast(F32)
        ysb = sb("ysb", [128, KD, TB], BF16)
        hid = sb("hid", [128, NF, TB], BF16)
        zr = [sb(f"zr{i}", [128, KD, 128], BF16) for i in range(NZ)]
        gur = [sb(f"gur{i}", [128, 2, KD, 128], BF16) for i in range(NR)]
        wdr = [sb(f"wdr{i}", [128, D], BF16) for i in range(ND)]
        ss = sb("ss", [128, 64], F32)
        pcorr = sb("pcorr", [128, 16], F32)
        ps = [es.enter_context(nc.psum_tensor(f"ps{i}", [128, TB], F32)) for i in range(8)]

        def sem(name):
            return es.enter_context(nc.semaphore(name))

        sems = {e: sem("sem_" + e) for e in Sched.ENGS}
        sc = Sched(sems)

        def dsem(name):
            return DSem(sem(name))

        ds_const = dsem("d_const")
        ds_w = dsem("d_w")
        ds_cv = [dsem(f"d_cv{i}") for i in range(7)]
        ds_x = [dsem(f"d_x{i}") for i in range(NH)]
        ds_u4 = [dsem(f"d_u4{i}") for i in range(4)]
        ds_o = [dsem(f"d_o{i}") for i in range(NH)]
        ds_z = [dsem(f"d_z{i}") for i in range(NZ)]
        ds_gu = [dsem(f"d_gu{i}") for i in range(NR)]
        ds_wd = [dsem(f"d_wd{i}") for i in range(ND)]

        b_const = Buf("vecs")
        b_gfin = Buf("gfin")
        b_invc = Buf("invc")
        b_eps = Buf("eps")
        b_ones = Buf("ones")
        b_wout = Buf("w_out")
        b_wpool = Buf("w_pool")
        b_ident = Buf("ident")
        b_wp = Buf("wp")
        b_wpB = Buf("wpB")
        b_stack = Buf("stack32")
        b_u4 = [[Buf(f"u4_{q}_{i}") for i in range(16)] for q in range(4)]
        b_hs = [Buf(f"hs{i}") for i in range(NH)]
        b_xs = [Buf(f"xs{i}") for i in range(2)]
        b_xnT = [Buf(f"xnT{m}") for m in range(4)]
        b_hnT = [Buf(f"hnT{m}") for m in range(4)]
        b_sig = [Buf(f"sig{i}") for i in range(2)]
        b_sgf = [Buf(f"sgf{i}") for i in range(2)]
        b_u = [Buf(f"u{q}") for q in range(4)]
        b_pb = [Buf(f"pb{q}") for q in range(4)]
        b_ptmp = [Buf(f"ptmp{i}") for i in range(2)]
        b_ypool = [Buf(f"ypool{q}") for q in range(4)]
        b_v = [Buf(f"v{q}") for q in range(4)]
        b_vbf = [Buf(f"vbf{i}") for i in range(2)]
        b_sqb = [Buf(f"sqb{i}") for i in range(2)]
        b_lnA = Buf("lnA")
        b_lnB = Buf("lnB")
        b_y = [Buf(f"y{j}") for j in range(KD)]
        b_hid = [Buf(f"hid{c}") for c in range(NF)]
        b_zr = [Buf(f"zr{i}") for i in range(NZ)]
        b_gur = [Buf(f"gurg{i}") for i in range(NR)]
        b_guru = [Buf(f"guru{i}") for i in range(NR)]
        b_wdr = [Buf(f"wdr{i}") for i in range(ND)]
        b_ss = [Buf(f"ss{i}") for i in range(64)]
        b_ps = [Buf(f"ps{i}") for i in range(8)]
        b_pcorr = Buf("pcorr")
        b_scr_in = Buf("scr_in")
        b_scr_g = [Buf("scr_g0"), Buf("scr_g1")]
        b_scr_u = [Buf("scr_u0"), Buf("scr_u1")]
        b_scr_d = [Buf("scr_d0"), Buf("scr_d1")]

        state = {"bank": 0, "ss": 0}
        held = set()

        def bank():
            while True:
                b = state["bank"] % 8
                state["bank"] += 1
                if b not in held:
                    return b

        def sscol():
            i = state["ss"] % 64
            state["ss"] += 1
            return i

        def vcol(c):
            return vecs[:, c:c + 1]

        sc.dma("sp", vecs[:], vecs_d, ds_const, writes=[b_const])
        sc.dma("sp", gfin[:], gfin_d, ds_const, writes=[b_gfin])
        sc.dma("sp", invc[:], invc_d, ds_const, writes=[b_invc])
        sc.dma("pool", ident[:], ident_d, ds_w, writes=[b_ident])
        sc.dma("pool", stack32[:], stack_d, ds_w, writes=[b_stack])
        sc.dma("pool", win_bf, w_in_d, ds_cv[0], writes=[b_scr_in])
        sc.dma("pool", w_pool_sb[:], w_pool_d.rearrange("i g h -> g i h"), ds_w, writes=[b_wpool])
        sc.dma("pool", w_out_sb[:], w_out_d.rearrange("(k p) n -> p k n", p=128), ds_w, writes=[b_wout])
        HF = DFF // 2
        for g in range(2):
            sc.dma("pool", wg_bf[:, g * HF:(g + 1) * HF], w_gate_d[:, g * HF:(g + 1) * HF], ds_cv[1 + g],
                   writes=[b_scr_g[g]])
            sc.dma("pool", wu_bf[:, g * HF:(g + 1) * HF], w_up_d[:, g * HF:(g + 1) * HF], ds_cv[3 + g],
                   writes=[b_scr_u[g]])
        for g in range(2):
            sc.dma("pool", wd_bf[g * HF:(g + 1) * HF, :], w_down_d[g * HF:(g + 1) * HF, :], ds_cv[5 + g],
                   writes=[b_scr_d[g]])

        win_v = win_bf.rearrange("(k p) n -> p k n", p=128)
        wg_v = wg_bf.rearrange("(k p) n -> p k n", p=128)
        wu_v = wu_bf.rearrange("(k p) n -> p k n", p=128)

        def load_z(g):
            if g >= NTILE * 12:
                return
            s_ = g % NZ
            j = ZORDER[g % 12]
            sc.dma("sp", zr[s_][:], win_v[:, :, j * 128:(j + 1) * 128], ds_z[s_], reads=[b_scr_in], writes=[b_zr[s_]])

        def load_gu(g):
            if g >= NTILE * NF:
                return
            s_ = g % NR
            c = g % NF
            hf = 0 if c < NF // 2 else 1
            sc.dma("sp", gur[s_][:, 0, :, :], wg_v[:, :, c * 128:(c + 1) * 128], ds_gu[s_],
                   reads=[b_scr_g[hf]], writes=[b_gur[s_]])
            sc.dma("sp", gur[s_][:, 1, :, :], wu_v[:, :, c * 128:(c + 1) * 128], ds_gu[s_],
                   reads=[b_scr_u[hf]], writes=[b_guru[s_]])

        def load_wd(g):
            if g >= NTILE * NF:
                return
            s_ = g % ND
            c = g % NF
            hf = 0 if c < NF // 2 else 1
            sc.dma("sp", wdr[s_][:], wd_bf[c * 128:(c + 1) * 128, :], ds_wd[s_], reads=[b_scr_d[hf]], writes=[b_wdr[s_]])

        def load_x(t, m):
            if t >= NTILE:
                return
            gm = 4 * t + m
            sl = gm % NH
            sc.dma("sp", hsl[sl][:], x_d[gm * 128:(gm + 1) * 128, :], ds_x[sl], writes=[b_hs[sl]])

        sc.op("dve", lambda h: h.memset(ones[:], 1.0), writes=[b_ones])
        sc.op("dve", lambda h: h.memset(epsr[:], RMS_EPS), writes=[b_eps])
        sc.op("dve", lambda h: h.memset(epsl[:], LN_EPS), writes=[b_eps])
        sc.op("dve", lambda h: h.memset(u[:], 0.0), writes=b_u)
        sc.op("dve", lambda h: h.memset(pb[:], 0.0), writes=b_pb)

        for m in range(4):
            load_x(0, m)
        for m in range(4):
            load_x(1, m)
        for q in range(4):
            for r in range(8):
                for g in range(4):
                    idx = q * 32 + r * 4 + g
                    if idx % 2 == 0:
                        sc.op("dve", lambda h, q=q, r=r, g=g, idx=idx: h.tensor_scalar(
                            out=wp[:, q, r, g, :], in0=stack32[:], scalar1=vcol(C_WDW + idx), scalar2=None,
                            op0=ALU.mult),
                            reads=[b_stack, b_const], writes=[b_wp])
                    else:
                        sc.op("act", lambda h, q=q, r=r, g=g, idx=idx: h.activation(
                            out=wp[:, q, r, g, :], in_=stack32[:], func=AF.Copy, scale=vcol(C_WDW + idx)),
                            reads=[b_stack, b_const], writes=[b_wpB])
        for g in range(NZ):
            load_z(g)
        for g in range(NR):
            load_gu(g)
        for g in range(ND):
            load_wd(g)

        gmix_b = vecs[:, C_GMIX:C_GMIX + KD].unsqueeze(2).to_broadcast([128, KD, 128])
        gffn_b = vecs[:, C_GFFN:C_GFFN + KD].unsqueeze(2).to_broadcast([128, KD, 128])

        lnA_bf = lnA_t[:]
        lnB_bf = lnB_t[:]
        sig_bf = [sig_t[i][:] for i in range(2)]
        import os
        if os.environ.get("KV_SLOTS", "1") == "1":
            SLOT_A = [(xs[0][:], b_xs[0]), (xs[1][:], b_xs[1]), (lnA_bf, b_lnA), (lnB_bf, b_lnB)]
            SLOT_H = [(xs[0][:], b_xs[0]), (xs[1][:], b_xs[1]), (sig_bf[0], b_sig[0]), (sig_bf[1], b_sig[1])]
        else:
            SLOT_A = [(xs[0][:], b_xs[0]), (xs[1][:], b_xs[1]), (xs[0][:], b_xs[0]), (xs[1][:], b_xs[1])]
            SLOT_H = SLOT_A

        def rms_stats(src_ap, src_buf, junk):
            junk_ap, junk_buf = junk
            ci = sscol()
            col = ss[:, ci:ci + 1]
            sc.op("act", lambda h: h.activation(out=junk_ap, in_=src_ap, func=AF.Square, accum_out=col),
                  reads=[src_buf], writes=[junk_buf, b_ss[ci]])
            sc.op("act", lambda h: h.activation(out=col, in_=col, func=AF.Sqrt, scale=1.0 / D, bias=epsr[:]),
                  reads=[b_ss[ci], b_eps], writes=[b_ss[ci]])
            sc.op("dve", lambda h: h.reciprocal(out=col, in_=col), reads=[b_ss[ci]], writes=[b_ss[ci]])
            return ci

        def rms_stats4(srcs, junks):
            while state["ss"] % 4 != 0:
                state["ss"] += 1
            c0 = state["ss"] % 64
            state["ss"] += 4
            bufs4 = [b_ss[c0 + i] for i in range(4)]
            for i in range(4):
                src_ap, src_buf = srcs[i]
                junk_ap, junk_buf = junks[i]
                col = ss[:, c0 + i:c0 + i + 1]
                sc.op("act", lambda h, junk_ap=junk_ap, src_ap=src_ap, col=col: h.activation(
                    out=junk_ap, in_=src_ap, func=AF.Square, accum_out=col),
                    reads=[src_buf], writes=[junk_buf, bufs4[i]])
            c4 = ss[:, c0:c0 + 4]
            sc.op("act", lambda h: h.activation(out=c4, in_=c4, func=AF.Sqrt, scale=1.0 / D, bias=epsr[:]),
                  reads=bufs4 + [b_eps], writes=bufs4)
            sc.op("dve", lambda h: h.reciprocal(out=c4, in_=c4), reads=bufs4, writes=bufs4)
            return c0

        def scaled_copy(t, m, slot):
            gm = 4 * t + m
            sl = gm % NH
            ci = rms_stats(hsl[sl][:], b_hs[sl], slot)
            sc.op("act", lambda h: h.activation(out=slot[0], in_=hsl[sl][:], func=AF.Copy, scale=ss[:, ci:ci + 1]),
                  reads=[b_hs[sl], b_ss[ci]], writes=[slot[1]])

        def transposes(slot, m, dstT, b_dst, gb):
            bk = bank()
            pT = ps[bk][:].bitcast(BF16).rearrange("p (k n) -> p k n", k=KD)
            src_ap, src_buf = slot

            def f(h):
                ins = None
                for kc in range(KD):
                    ins = h.transpose(out=pT[:, kc, :], in_=src_ap[:, kc * 128:(kc + 1) * 128], identity=ident[:])
                return ins
            sc.op("pe", f, reads=[src_buf, b_ident], writes=[b_ps[bk]])
            sc.op("dve", lambda h: h.tensor_tensor(out=dstT[:, :, m * 128:(m + 1) * 128], in0=pT, in1=gb, op=ALU.mult),
                  reads=[b_ps[bk], b_const], writes=[b_dst[m]])

        def mm_z(t, zi):
            g = t * 12 + zi
            s_ = g % NZ
            bk = bank()

            def f(h):
                ins = None
                for kc in range(KD):
                    ins = h.matmul(ps[bk][:], lhsT=zr[s_][:, kc, :], rhs=xnT[:, kc, :],
                                   start=(kc == 0), stop=(kc == KD - 1))
                return ins
            sc.op("pe", f, reads=b_xnT + [b_zr[s_]], writes=[b_ps[bk]])
            load_z(g + NZ)
            return bk

        def mixer_gen(t):
            srcs = [(hsl[(4 * t + m) % NH][:], b_hs[(4 * t + m) % NH]) for m in range(4)]
            c0 = rms_stats4(srcs, SLOT_A)
            yield 1
            for m in range(4):
                sc.op("act", lambda h, m=m, c0=c0, src=srcs[m][0]: h.activation(
                    out=SLOT_A[m][0], in_=src, func=AF.Copy, scale=ss[:, c0 + m:c0 + m + 1]),
                    reads=[srcs[m][1], b_ss[c0 + m]], writes=[SLOT_A[m][1]])
            yield 1
            for m in range(4):
                transposes(SLOT_A[m], m, xnT, b_xnT, gmix_b)
                yield (1 if m == 3 else 0)
            zi = 0
            for q in range(4):
                bg = mm_z(t, zi)
                zi += 1
                si = q % 2
                sc.op("act", lambda h, bg=bg, si=si, q=q: h.activation(
                    out=sig[si][:], in_=ps[bg][:], func=AF.Sigmoid, bias=vcol(C_BIN + 4 + q)),
                    reads=[b_ps[bg], b_const], writes=[b_sig[si]])
                yield 0
                ba = mm_z(t, zi)
                zi += 1
                sc.op("dve", lambda h, ba=ba, si=si, q=q: h.scalar_tensor_tensor(
                    out=u[:, q, 31:31 + TB], in0=ps[ba][:], scalar=vcol(C_BIN + q), in1=sig[si][:],
                    op0=ALU.add, op1=ALU.mult),
                    reads=[b_ps[ba], b_sig[si], b_const], writes=[b_u[q]])
                for g in range(4):
                    for s_ in range(4):
                        sc.dma("sp", u4[q][32 * s_:32 * s_ + 32, g, 0:519],
                               u[32 * g:32 * g + 32, q, 24 - 8 * s_:24 - 8 * s_ + 519], ds_u4[q],
                               reads=[b_u[q]], writes=[b_u4[q][g * 4 + s_]])
                yield (1 if q % 2 == 1 else 0)
            for i in range(4):
                bp = mm_z(t, zi)
                zi += 1
                sc.op("act", lambda h, bp=bp, i=i: h.activation(
                    out=pb[:, i, 16:16 + TB], in_=ps[bp][:], func=AF.Identity, bias=vcol(C_BIN + 8 + i)),
                    reads=[b_ps[bp], b_const], writes=[b_pb[i]])
                w = 2 ** (i + 1)
                src = pb[:, i, :]
                srcb = b_pb[i]
                lo = 0
                for k in range(i + 1):
                    sh = 2 ** k
                    dst = ptmp[k % 2][:]
                    dstb = b_ptmp[k % 2]
                    lo2 = lo + sh
                    sc.op("dve", lambda h, dst=dst, src=src, lo2=lo2, sh=sh: h.tensor_tensor(
                        out=dst[:, lo2:528], in0=src[:, lo2:528], in1=src[:, lo2 - sh:528 - sh], op=ALU.add),
                        reads=[srcb], writes=[dstb])
                    src, srcb, lo = dst, dstb, lo2
                sc.op("dve", lambda h, src=src, i=i, w=w: h.scalar_tensor_tensor(
                    out=ypool[:, i, :], in0=src[:, 16:16 + TB], scalar=1.0 / w, in1=pb[:, i, 16:16 + TB],
                    op0=ALU.mult, op1=ALU.subtract),
                    reads=[srcb, b_pb[i]], writes=[b_ypool[i]])
                if t == 0:
                    n = w - 1
                    sc.op("dve", lambda h, src=src, n=n: h.tensor_tensor(
                        out=pcorr[:, 0:n], in0=src[:, 16:16 + n], in1=invc[:, 0:n], op=ALU.mult),
                        reads=[srcb, b_invc], writes=[b_pcorr])
                    sc.op("dve", lambda h, i=i, n=n: h.tensor_tensor(
                        out=ypool[:, i, 0:n], in0=pcorr[:, 0:n], in1=pb[:, i, 16:16 + n], op=ALU.subtract),
                        reads=[b_pcorr, b_pb[i]], writes=[b_ypool[i]])
                sc.op("dve", lambda h, i=i: h.tensor_copy(out=pb[:, i, 0:16], in_=pb[:, i, TB:TB + 16]),
                      reads=[b_pb[i]], writes=[b_pb[i]])
                yield (1 if i == 3 else 0)
            bS1 = bank()
            held.add(bS1)
            bS2 = bank()
            held.add(bS2)
            for q in range(4):
                bc = bank()

                def fconv(h, q=q, bc=bc):
                    ins = None
                    for r in range(8):
                        for g in range(4):
                            ins = h.matmul(ps[bc][32 * g:32 * g + 32, :], lhsT=wp[:, q, r, g, :],
                                           rhs=u4[q][:, g, 7 - r:7 - r + TB],
                                           start=(r == 0), stop=(r == 7), tile_position=(0, 32 * g))
                    return ins
                sc.op("pe", fconv, reads=b_u4[q] + [b_wp, b_wpB], writes=[b_ps[bc]])
                sc.op("dve", lambda h, q=q: h.tensor_copy(out=u[:, q, 0:31], in_=u[:, q, TB:TB + 31]),
                      reads=[b_u[q]], writes=[b_u[q]])
                vi = q % 2
                sc.op("act", lambda h, q=q, bc=bc: h.activation(
                    out=v[:, q, :], in_=ps[bc][:], func=AF.Identity, bias=vcol(C_BDW + q)),
                    reads=[b_ps[bc], b_const], writes=[b_v[q]])
                sc.op("act", lambda h, q=q, bc=bc, vi=vi: h.activation(
                    out=vbf[vi][:], in_=ps[bc][:], func=AF.Identity, bias=vcol(C_BDW + q)),
                    reads=[b_ps[bc], b_const], writes=[b_vbf[vi]])
                sc.op("act", lambda h, q=q, bc=bc, vi=vi: h.activation(
                    out=sqb[vi][:], in_=ps[bc][:], func=AF.Square, bias=vcol(C_BDW + q)),
                    reads=[b_ps[bc], b_const], writes=[b_sqb[vi]])
                yield 1
                sc.op("pe", lambda h, q=q, vi=vi: h.matmul(ps[bS1][:], lhsT=ones[:], rhs=vbf[vi][:],
                                                           start=(q == 0), stop=(q == 3)),
                      reads=[b_vbf[vi], b_ones], writes=[b_ps[bS1]])
                sc.op("pe", lambda h, q=q, vi=vi: h.matmul(ps[bS2][:], lhsT=ones[:], rhs=sqb[vi][:],
                                                           start=(q == 0), stop=(q == 3)),
                      reads=[b_sqb[vi], b_ones], writes=[b_ps[bS2]])
                if q == 1:
                    for i in range(4):
                        bq = bank()
                        sc.op("pe", lambda h, i=i, bq=bq: h.matmul(ps[bq][:], lhsT=w_pool_sb[:, i, :],
                                                                   rhs=ypool[:, i, :], start=True, stop=True),
                              reads=[b_ypool[i], b_wpool], writes=[b_ps[bq]])
                        sc.op("act", lambda h, i=i, bq=bq: h.activation(
                            out=ysb[:, 4 + i, :], in_=ps[bq][:], func=AF.Copy, scale=vcol(C_SPOOL + i)),
                            reads=[b_ps[bq], b_const], writes=[b_y[4 + i]])
                yield 0
            sc.op("act", lambda h: h.activation(out=lnA[:], in_=ps[bS1][:], func=AF.Square, scale=1.0 / 512),
                  reads=[b_ps[bS1]], writes=[b_lnA])
            sc.op("dve", lambda h: h.scalar_tensor_tensor(out=lnA[:], in0=ps[bS2][:], scalar=1.0 / 512, in1=lnA[:],
                                                          op0=ALU.mult, op1=ALU.subtract),
                  reads=[b_ps[bS2], b_lnA], writes=[b_lnA])
            sc.op("act", lambda h: h.activation(out=lnA[:], in_=lnA[:], func=AF.Sqrt, bias=epsl[:]),
                  reads=[b_lnA, b_eps], writes=[b_lnA])
            sc.op("dve", lambda h: h.reciprocal(out=lnA[:], in_=lnA[:]), reads=[b_lnA], writes=[b_lnA])
            sc.op("dve", lambda h: h.scalar_tensor_tensor(out=lnB[:], in0=ps[bS1][:], scalar=-1.0 / 512, in1=lnA[:],
                                                          op0=ALU.mult, op1=ALU.mult),
                  reads=[b_ps[bS1], b_lnA], writes=[b_lnB])
            held.discard(bS1)
            held.discard(bS2)
            for q in range(4):
                sc.op("dve", lambda h, q=q: h.tensor_tensor(out=v[:, q, :], in0=v[:, q, :], in1=lnA[:], op=ALU.mult),
                      reads=[b_v[q], b_lnA], writes=[b_v[q]])
                sc.op("dve", lambda h, q=q: h.tensor_tensor(out=v[:, q, :], in0=v[:, q, :], in1=lnB[:], op=ALU.add),
                      reads=[b_v[q], b_lnB], writes=[b_v[q]])
                sc.op("act", lambda h, q=q: h.activation(out=ysb[:, q, :], in_=v[:, q, :], func=AF.Silu,
                                                         scale=vcol(C_LNG + q), bias=vcol(C_LNB + q)),
                      reads=[b_v[q], b_const], writes=[b_y[q]])
            yield 5
            for m in range(4):
                gm = 4 * t + m
                sl = gm % NH
                for n in range(2):
                    bk = bank()

                    def f(h, m=m, n=n, bk=bk):
                        ins = None
                        for kc in range(KD):
                            ins = h.matmul(ps[bk][:], lhsT=ysb[:, kc, m * 128:(m + 1) * 128],
                                           rhs=w_out_sb[:, kc, n * TB:(n + 1) * TB],
                                           start=(kc == 0), stop=(kc == KD - 1))
                        return ins
                    sc.op("pe", f, reads=b_y + [b_wout], writes=[b_ps[bk]])
                    sc.op("dve", lambda h, sl=sl, n=n, bk=bk: h.tensor_tensor(
                        out=hsl[sl][:, n * TB:(n + 1) * TB], in0=ps[bk][:], in1=hsl[sl][:, n * TB:(n + 1) * TB],
                        op=ALU.add),
                        reads=[b_ps[bk], b_hs[sl]], writes=[b_hs[sl]])
                    yield 0
                scaled_copy(t, m, SLOT_H[m])
            yield 2

        def hn_gen(t):
            for m in range(4):
                transposes(SLOT_H[m], m, hnT, b_hnT, gffn_b)
                yield 0

        def ffn_up_gen(t):
            for c in range(NF):
                g = t * NF + c
                s_ = g % NR
                bg = bank()
                bu = bank()

                def fg(h, s_=s_, bg=bg):
                    ins = None
                    for kc in range(KD):
                        ins = h.matmul(ps[bg][:], lhsT=gur[s_][:, 0, kc, :], rhs=hnT[:, kc, :],
                                       start=(kc == 0), stop=(kc == KD - 1))
                    return ins

                def fu(h, s_=s_, bu=bu):
                    ins = None
                    for kc in range(KD):
                        ins = h.matmul(ps[bu][:], lhsT=gur[s_][:, 1, kc, :], rhs=hnT[:, kc, :],
                                       start=(kc == 0), stop=(kc == KD - 1))
                    return ins
                sc.op("pe", fg, reads=b_hnT + [b_gur[s_]], writes=[b_ps[bg]])
                sc.op("pe", fu, reads=b_hnT + [b_guru[s_]], writes=[b_ps[bu]])
                si = c % 2
                sc.op("act", lambda h, bg=bg, si=si: h.activation(out=sgf[si][:], in_=ps[bg][:], func=AF.Silu),
                      reads=[b_ps[bg]], writes=[b_sgf[si]])
                sc.op("dve", lambda h, bu=bu, si=si, c=c: h.tensor_tensor(
                    out=hid[:, c, :], in0=ps[bu][:], in1=sgf[si][:], op=ALU.mult),
                    reads=[b_ps[bu], b_sgf[si]], writes=[b_hid[c]])
                load_gu(g + NR)
                yield 1

        def ffn_down(t):
            for c in range(NF):
                g = t * NF + c
                s_ = g % ND

                def f(h, c=c, s_=s_):
                    ins = None
                    for m in range(4):
                        for n in range(2):
                            ins = h.matmul(ps[m * 2 + n][:], lhsT=hid[:, c, m * 128:(m + 1) * 128],
                                           rhs=wdr[s_][:, n * TB:(n + 1) * TB],
                                           start=(c == 0), stop=(c == NF - 1))
                    return ins
                sc.op("pe", f, reads=[b_hid[c], b_wdr[s_]], writes=b_ps)
                load_wd(g + ND)
            for m in range(4):
                gm = 4 * t + m
                sl = gm % NH
                for n in range(2):
                    bk = m * 2 + n
                    sc.op("dve", lambda h, sl=sl, n=n, bk=bk: h.tensor_tensor(
                        out=hsl[sl][:, n * TB:(n + 1) * TB], in0=ps[bk][:], in1=hsl[sl][:, n * TB:(n + 1) * TB],
                        op=ALU.add),
                        reads=[b_ps[bk], b_hs[sl]], writes=[b_hs[sl]])
            state["bank"] = 0

        def final_gen(t):
            yield 1
            srcs = [(hsl[(4 * t + m) % NH][:], b_hs[(4 * t + m) % NH]) for m in range(4)]
            c0 = rms_stats4(srcs, SLOT_H)
            yield 1
            for m in range(4):
                gm = 4 * t + m
                sl = gm % NH
                sc.op("dve", lambda h, sl=sl, ci=c0 + m: h.scalar_tensor_tensor(
                    out=hsl[sl][:], in0=hsl[sl][:], scalar=ss[:, ci:ci + 1], in1=gfin[:],
                    op0=ALU.mult, op1=ALU.mult),
                    reads=[b_hs[sl], b_ss[c0 + m], b_gfin], writes=[b_hs[sl]])
            for m in range(4):
                gm = 4 * t + m
                sl = gm % NH
                sc.dma("sp", y_d[gm * 128:(gm + 1) * 128, :], hsl[sl][:], ds_o[sl], reads=[b_hs[sl]])
            for m in range(4):
                load_x(t + 2, m)
            yield 3

        def run(gen):
            for _ in gen:
                pass

        def chain(*gens):
            for g_ in gens:
                yield from g_

        def merge2(gf, gm):
            for k in gm:
                for _ in range(k):
                    next(gf, None)
            for _ in gf:
                pass

        run(mixer_gen(0))
        run(hn_gen(0))
        for t in range(NTILE):
            gens = []
            if t >= 1:
                gens.append(final_gen(t - 1))
            if t + 1 < NTILE:
                gens.append(mixer_gen(t + 1))
            merge2(ffn_up_gen(t), chain(*gens))
            if t + 1 < NTILE:
                run(hn_gen(t + 1))
            ffn_down(t)
        run(final_gen(NTILE - 1))
        for i in range(NH):
            sc.wait_dma("sp", ds_o[i])

        with nc.Block() as block:
            @block.tensor
            def _(h):
                sc.emit("pe", h)

            @block.scalar
            def _(h):
                sc.emit("act", h)

            @block.vector
            def _(h):
                sc.emit("dve", h)

            @block.gpsimd
            def _(h):
                sc.emit("pool", h)

            @block.sync
            def _(h):
                sc.emit("sp", h)
    nc._sched = sc
    return nc


def _wtab(w_dw):
    wpad = np.concatenate([w_dw, np.zeros((1, 512), np.float32)], axis=0)
    out = np.zeros((128, 4, 8, 4), np.float32)
    for s_ in range(4):
        for r in range(8):
            k = 30 - 8 * s_ - r
            row = wpad[k] if k >= 0 else wpad[31]
            out[32 * s_:32 * s_ + 32, :, r, :] = row.reshape(4, 4, 32).transpose(2, 0, 1)
    return out.reshape(128, 128)


def _prep_shared(inp):
    f = np.float32
    b_in = np.asarray(inp["b_in"], f).reshape(DIN)
    cols = [
        b_in.reshape(12, 128).T,
        np.asarray(inp["g_mix"], f).reshape(8, 128).T,
        np.asarray(inp["g_ffn"], f).reshape(8, 128).T,
        np.asarray(inp["b_dw"], f).reshape(4, 128).T,
        np.asarray(inp["ln_g"], f).reshape(4, 128).T,
        np.asarray(inp["ln_b"], f).reshape(4, 128).T,
        np.asarray(inp["s_pool"], f).reshape(4, 128).T,
        _wtab(np.asarray(inp["w_dw"], f).reshape(CW, 512)),
    ]
    vecs = np.ascontiguousarray(np.concatenate(cols, axis=1), dtype=f)
    assert vecs.shape == (128, NV)
    shared = {
        "w_in": np.ascontiguousarray(np.asarray(inp["w_in"], f).reshape(D, DIN)),
        "w_out": np.ascontiguousarray(np.asarray(inp["w_out"], f).reshape(D, D)),
        "w_pool": np.ascontiguousarray(np.asarray(inp["w_pool"], f).reshape(4, 128, 128)),
        "w_gate": np.ascontiguousarray(np.asarray(inp["w_gate"], f).reshape(D, DFF)),
        "w_up": np.ascontiguousarray(np.asarray(inp["w_up"], f).reshape(D, DFF)),
        "w_down": np.ascontiguousarray(np.asarray(inp["w_down"], f).reshape(DFF, D)),
        "vecs": vecs,
        "gfin": np.ascontiguousarray(np.broadcast_to(np.asarray(inp["g_final"], f).reshape(1, D), (128, D))),
        "ident": np.eye(128, dtype=f),
        "stack32": np.ascontiguousarray(np.tile(np.eye(32, dtype=f), (4, 1))),
        "invc": np.ascontiguousarray(np.broadcast_to((1.0 / np.arange(1, 17, dtype=f)).reshape(1, 16), (128, 16))),
    }
    return shared


_NC_CACHE = {}


def kernel(**inputs):
    x = np.asarray(inputs["x"], np.float32)
    shared = _prep_shared(inputs)
    if "nc" not in _NC_CACHE:
        _NC_CACHE["nc"] = build_nc()
    nc = _NC_CACHE["nc"]
    in_maps = []
    for c in range(NCORES):
        m = dict(shared)
        m["x"] = np.ascontiguousarray(x[c])
        in_maps.append(m)
    res = run_bass_kernel_spmd(nc, in_maps, core_ids=list(range(NCORES)))
    out = np.stack([np.asarray(res.results[c]["y"], np.float32) for c in range(NCORES)], axis=0)
    return out
```

```python
import numpy as np
import concourse.bass as bass
import concourse.mybir as mybir
from concourse.bass_utils import run_bass_kernel_spmd

F32 = mybir.dt.float32
BF16 = mybir.dt.bfloat16
ALU = mybir.AluOpType
AF = mybir.ActivationFunctionType

NCORES = 8
S = 4096
D = 1024
TB = 512
NTILE = S // TB
KD = D // 128
CW = 31
DIN = 1536
DFF = 2816
NF = DFF // 128
RMS_EPS = 1e-6
LN_EPS = 1e-5
NH = 8
NR = 3
ND = 4
NZ = 3

C_BIN = 0
C_GMIX = 12
C_GFFN = 20
C_BDW = 28
C_LNG = 32
C_LNB = 36
C_SPOOL = 40
C_WDW = 44
NV = 44 + 128


class Buf:
    __slots__ = ("name", "w", "r")

    def __init__(self, name):
        self.name = name
        self.w = None
        self.r = {}


class DSem:
    def __init__(self, handle):
        self.handle = handle
        self.count = 0


class Sched:
    ENGS = ("pe", "act", "dve", "pool", "sp")

    def __init__(self, sems):
        self.sem = sems
        self.tick = {e: 0 for e in self.ENGS}
        self.waited = {e: {} for e in self.ENGS}
        self.prog = {e: [] for e in self.ENGS}
        self.log = {e: [] for e in self.ENGS}
        self.needed = {e: set() for e in self.ENGS}

    def _wait(self, e, dep):
        kind, key, val = dep
        if kind == "e":
            if key == e and e in ("pe", "sp"):
                return
            k = ("e", key)
            semh = self.sem[key]
        else:
            k = ("d", id(key))
            semh = key.handle
            val = key.count
        w = self.waited[e]
        if w.get(k, 0) >= val:
            return
        w[k] = val
        self.log[e].append(("wait", k, val))
        if kind == "e":
            self.needed[key].add(val)
            self.prog[e].append(("ewait", key, val))
        else:
            self.prog[e].append(("dwait", semh, val))

    def _deps(self, e, reads, writes):
        for b in reads:
            if b.w is not None:
                self._wait(e, b.w)
        for b in writes:
            if b.w is not None:
                self._wait(e, b.w)
            for d in list(b.r.values()):
                self._wait(e, d)

    def op(self, e, fn, reads=(), writes=()):
        self._deps(e, reads, writes)
        self.tick[e] += 1
        me = ("e", e, self.tick[e])
        semh = self.sem[e]
        self.log[e].append(("inc", ("e", e), 1))
        self.prog[e].append(("op", fn, self.tick[e]))
        for b in reads:
            b.r[("e", e)] = me
        for b in writes:
            b.w = me
            b.r = {}

    def dma(self, q, out, in_, dsem, reads=(), writes=()):
        self._deps(q, reads, writes)
        dsem.count += 16
        me = ("d", dsem, dsem.count)
        self.log[q].append(("inc", ("d", id(dsem)), 16))
        self.prog[q].append(("dma", out, in_, dsem.handle))
        for b in reads:
            b.r[("d", id(dsem))] = me
        for b in writes:
            b.w = me
            b.r = {}

    def emit(self, e, h):
        import bisect
        rank = {f: sorted(self.needed[f]) for f in self.ENGS}
        for item in self.prog[e]:
            kind = item[0]
            if kind == "op":
                _, fn, tick = item
                ins = fn(h)
                if tick in self.needed[e]:
                    ins.then_inc(self.sem[e], 1)
            elif kind == "ewait":
                _, f, val = item
                h.wait_ge(self.sem[f], bisect.bisect_right(rank[f], val))
            elif kind == "dwait":
                _, semh, val = item
                h.wait_ge(semh, val)
            else:
                _, o, i, semh = item
                h.dma_start(out=o, in_=i).then_inc(semh, 16)

    def wait_dma(self, e, dsem):
        self._wait(e, ("d", dsem, dsem.count))


ZORDER = [4, 0, 5, 1, 6, 2, 7, 3, 8, 9, 10, 11]


def build_nc():
    nc = bass.Bass("TRN2", target_bir_lowering=False)

    def din(name, shape):
        return nc.dram_tensor(name, list(shape), F32, kind="ExternalInput").ap()

    x_d = din("x", [S, D])
    w_in_d = din("w_in", [D, DIN])
    w_out_d = din("w_out", [D, D])
    w_pool_d = din("w_pool", [4, 128, 128])
    w_gate_d = din("w_gate", [D, DFF])
    w_up_d = din("w_up", [D, DFF])
    w_down_d = din("w_down", [DFF, D])
    vecs_d = din("vecs", [128, NV])
    gfin_d = din("gfin", [128, D])
    ident_d = din("ident", [128, 128])
    invc_d = din("invc", [128, 16])
    stack_d = din("stack32", [128, 32])
    y_d = nc.dram_tensor("y", [S, D], F32, kind="ExternalOutput").ap()
    win_bf = nc.dram_tensor("win_bf", [D, DIN], BF16, kind="Internal").ap()
    wg_bf = nc.dram_tensor("wg_bf", [D, DFF], BF16, kind="Internal").ap()
    wu_bf = nc.dram_tensor("wu_bf", [D, DFF], BF16, kind="Internal").ap()
    wd_bf = nc.dram_tensor("wd_bf", [DFF, D], BF16, kind="Internal").ap()

    from contextlib import ExitStack
    with ExitStack() as es:
        def sb(name, shape, dt):
            return es.enter_context(nc.sbuf_tensor(name, list(shape), dt))

        w_out_sb = sb("w_out_sb", [128, KD, D], BF16)
        w_pool_sb = sb("w_pool_sb", [128, 4, 128], BF16)
        wp = sb("wp", [128, 4, 8, 4, 32], BF16)
        stack32 = sb("stack32_sb", [128, 32], BF16)
        u4 = [sb(f"u4_{q}", [128, 4, 528], BF16) for q in range(4)]
        ident = sb("ident_sb", [128, 128], BF16)
        ones = sb("ones_sb", [128, 128], BF16)
        vecs = sb("vecs_sb", [128, NV], F32)
        gfin = sb("gfin_sb", [128, D], F32)
        invc = sb("invc_sb", [128, 16], F32)
        epsr = sb("epsr", [128, 1], F32)
        epsl = sb("epsl", [128, 1], F32)
        hsl = [sb(f"hs{i}", [128, D], F32) for i in range(NH)]
        xs = [sb(f"xs{i}", [128, D], BF16) for i in range(2)]
        xnT = sb("xnT", [128, KD, TB], BF16)
        hnT = sb("hnT", [128, KD, TB], BF16)
        sig_t = [sb(f"sig{i}", [128, 2 * TB], BF16) for i in range(2)]
        sig = [sig_t[i][:].bitcast(F32) for i in range(2)]
        sgf = [sb(f"sgf{i}", [128, TB], F32) for i in range(2)]
        u = sb("u", [128, 4, 544], BF16)
        pb = sb("pb", [128, 4, 528], F32)
        ptmp = [sb(f"ptmp{i}", [128, 528], F32) for i in range(2)]
        ypool = sb("ypool", [128, 4, TB], BF16)
        v = sb("v", [128, 4, TB], F32)
        vbf = [sb(f"vbf{i}", [128, TB], BF16) for i in range(2)]
        sqb = [sb(f"sqb{i}", [128, TB], BF16) for i in range(2)]
        lnA_t = sb("lnA", [128, 2 * TB], BF16)
        lnB_t = sb("lnB", [128, 2 * TB], BF16)
        lnA = lnA_t[:].bitcast(F32)
        lnB = lnB_t[:].bitcast(F32)
        ysb = sb("ysb", [128, KD, TB], BF16)
        hid = sb("hid", [128, NF, TB], BF16)
        zr = [sb(f"zr{i}", [128, KD, 128], BF16) for i in range(NZ)]
        gur = [sb(f"gur{i}", [128, 2, KD, 128], BF16) for i in range(NR)]
        wdr = [sb(f"wdr{i}", [128, D], BF16) for i in range(ND)]
        ss = sb("ss", [128, 64], F32)
        pcorr = sb("pcorr", [128, 16], F32)
        ps = [es.enter_context(nc.psum_tensor(f"ps{i}", [128, TB], F32)) for i in range(8)]

        def sem(name):
            return es.enter_context(nc.semaphore(name))

        sems = {e: sem("sem_" + e) for e in Sched.ENGS}
        sc = Sched(sems)

        def dsem(name):
            return DSem(sem(name))

        ds_const = dsem("d_const")
        ds_w = dsem("d_w")
        ds_cv = [dsem(f"d_cv{i}") for i in range(7)]
        ds_x = [dsem(f"d_x{i}") for i in range(NH)]
        ds_u4 = [dsem(f"d_u4{i}") for i in range(4)]
        ds_o = [dsem(f"d_o{i}") for i in range(NH)]
        ds_z = [dsem(f"d_z{i}") for i in range(NZ)]
        ds_gu = [dsem(f"d_gu{i}") for i in range(NR)]
        ds_wd = [dsem(f"d_wd{i}") for i in range(ND)]

        b_const = Buf("vecs")
        b_gfin = Buf("gfin")
        b_invc = Buf("invc")
        b_eps = Buf("eps")
        b_ones = Buf("ones")
        b_wout = Buf("w_out")
        b_wpool = Buf("w_pool")
        b_ident = Buf("ident")
        b_wp = Buf("wp")
        b_wpB = Buf("wpB")
        b_stack = Buf("stack32")
        b_u4 = [[Buf(f"u4_{q}_{i}") for i in range(4)] for q in range(4)]
        b_hs = [Buf(f"hs{i}") for i in range(NH)]
        b_xs = [Buf(f"xs{i}") for i in range(2)]
        b_xnT = [Buf(f"xnT{m}") for m in range(4)]
        b_hnT = [Buf(f"hnT{m}") for m in range(4)]
        b_sig = [Buf(f"sig{i}") for i in range(2)]
        b_sgf = [Buf(f"sgf{i}") for i in range(2)]
        b_u = [Buf(f"u{q}") for q in range(4)]
        b_pb = [Buf(f"pb{q}") for q in range(4)]
        b_ptmp = [Buf(f"ptmp{i}") for i in range(2)]
        b_ypool = [Buf(f"ypool{q}") for q in range(4)]
        b_v = [Buf(f"v{q}") for q in range(4)]
        b_vbf = [Buf(f"vbf{i}") for i in range(2)]
        b_sqb = [Buf(f"sqb{i}") for i in range(2)]
        b_lnA = Buf("lnA")
        b_lnB = Buf("lnB")
        b_y = [Buf(f"y{j}") for j in range(KD)]
        b_hid = [Buf(f"hid{c}") for c in range(NF)]
        b_zr = [Buf(f"zr{i}") for i in range(NZ)]
        b_gur = [Buf(f"gurg{i}") for i in range(NR)]
        b_guru = [Buf(f"guru{i}") for i in range(NR)]
        b_wdr = [Buf(f"wdr{i}") for i in range(ND)]
        b_ss = [Buf(f"ss{i}") for i in range(64)]
        b_ps = [Buf(f"ps{i}") for i in range(8)]
        b_pcorr = Buf("pcorr")
        b_scr_in = Buf("scr_in")
        b_scr_g = [Buf("scr_g0"), Buf("scr_g1")]
        b_scr_u = [Buf("scr_u0"), Buf("scr_u1")]
        b_scr_d = [Buf("scr_d0"), Buf("scr_d1")]

        state = {"bank": 0, "ss": 0}
        held = set()

        def bank():
            while True:
                b = state["bank"] % 8
                state["bank"] += 1
                if b not in held:
                    return b

        def sscol():
            i = state["ss"] % 64
            state["ss"] += 1
            return i

        def vcol(c):
            return vecs[:, c:c + 1]

        sc.dma("sp", vecs[:], vecs_d, ds_const, writes=[b_const])
        sc.dma("sp", gfin[:], gfin_d, ds_const, writes=[b_gfin])
        sc.dma("sp", invc[:], invc_d, ds_const, writes=[b_invc])
        sc.dma("pool", ident[:], ident_d, ds_w, writes=[b_ident])
        sc.dma("pool", stack32[:], stack_d, ds_w, writes=[b_stack])
        sc.dma("pool", win_bf, w_in_d, ds_cv[0], writes=[b_scr_in])
        sc.dma("pool", w_pool_sb[:], w_pool_d.rearrange("i g h -> g i h"), ds_w, writes=[b_wpool])
        sc.dma("pool", w_out_sb[:], w_out_d.rearrange("(k p) n -> p k n", p=128), ds_w, writes=[b_wout])
        HF = DFF // 2
        for g in range(2):
            sc.dma("pool", wg_bf[:, g * HF:(g + 1) * HF], w_gate_d[:, g * HF:(g + 1) * HF], ds_cv[1 + g],
                   writes=[b_scr_g[g]])
            sc.dma("pool", wu_bf[:, g * HF:(g + 1) * HF], w_up_d[:, g * HF:(g + 1) * HF], ds_cv[3 + g],
                   writes=[b_scr_u[g]])
        for g in range(2):
            sc.dma("pool", wd_bf[g * HF:(g + 1) * HF, :], w_down_d[g * HF:(g + 1) * HF, :], ds_cv[5 + g],
                   writes=[b_scr_d[g]])

        win_v = win_bf.rearrange("(k p) n -> p k n", p=128)
        wg_v = wg_bf.rearrange("(k p) n -> p k n", p=128)
        wu_v = wu_bf.rearrange("(k p) n -> p k n", p=128)

        def load_z(g):
            if g >= NTILE * 12:
                return
            s_ = g % NZ
            j = ZORDER[g % 12]
            sc.dma("sp", zr[s_][:], win_v[:, :, j * 128:(j + 1) * 128], ds_z[s_], reads=[b_scr_in], writes=[b_zr[s_]])

        def load_gu(g):
            if g >= NTILE * NF:
                return
            s_ = g % NR
            c = g % NF
            hf = 0 if c < NF // 2 else 1
            sc.dma("sp", gur[s_][:, 0, :, :], wg_v[:, :, c * 128:(c + 1) * 128], ds_gu[s_],
                   reads=[b_scr_g[hf]], writes=[b_gur[s_]])
            sc.dma("sp", gur[s_][:, 1, :, :], wu_v[:, :, c * 128:(c + 1) * 128], ds_gu[s_],
                   reads=[b_scr_u[hf]], writes=[b_guru[s_]])

        def load_wd(g):
            if g >= NTILE * NF:
                return
            s_ = g % ND
            c = g % NF
            hf = 0 if c < NF // 2 else 1
            sc.dma("sp", wdr[s_][:], wd_bf[c * 128:(c + 1) * 128, :], ds_wd[s_], reads=[b_scr_d[hf]], writes=[b_wdr[s_]])

        def load_x(t, m):
            if t >= NTILE:
                return
            gm = 4 * t + m
            sl = gm % NH
            sc.dma("sp", hsl[sl][:], x_d[gm * 128:(gm + 1) * 128, :], ds_x[sl], writes=[b_hs[sl]])

        sc.op("dve", lambda h: h.memset(ones[:], 1.0), writes=[b_ones])
        sc.op("dve", lambda h: h.memset(epsr[:], RMS_EPS), writes=[b_eps])
        sc.op("dve", lambda h: h.memset(epsl[:], LN_EPS), writes=[b_eps])
        sc.op("dve", lambda h: h.memset(u[:], 0.0), writes=b_u)
        sc.op("dve", lambda h: h.memset(pb[:], 0.0), writes=b_pb)
        for q in range(4):
            sc.op("dve", lambda h, q=q: h.memset(u4[q][:], 0.0), writes=b_u4[q])

        for m in range(4):
            load_x(0, m)
        for m in range(4):
            load_x(1, m)
        for q in range(4):
            for r in range(8):
                for g in range(4):
                    idx = q * 32 + r * 4 + g
                    if idx % 2 == 0:
                        sc.op("dve", lambda h, q=q, r=r, g=g, idx=idx: h.tensor_scalar(
                            out=wp[:, q, r, g, :], in0=stack32[:], scalar1=vcol(C_WDW + idx), scalar2=None,
                            op0=ALU.mult),
                            reads=[b_stack, b_const], writes=[b_wp])
                    else:
                        sc.op("act", lambda h, q=q, r=r, g=g, idx=idx: h.activation(
                            out=wp[:, q, r, g, :], in_=stack32[:], func=AF.Copy, scale=vcol(C_WDW + idx)),
                            reads=[b_stack, b_const], writes=[b_wpB])
        for g in range(NZ):
            load_z(g)
        for g in range(NR):
            load_gu(g)
        for g in range(ND):
            load_wd(g)

        gmix_b = vecs[:, C_GMIX:C_GMIX + KD].unsqueeze(2).to_broadcast([128, KD, 128])
        gffn_b = vecs[:, C_GFFN:C_GFFN + KD].unsqueeze(2).to_broadcast([128, KD, 128])

        lnA_bf = lnA_t[:]
        lnB_bf = lnB_t[:]
        sig_bf = [sig_t[i][:] for i in range(2)]
        import os
        if os.environ.get("KV_SLOTS", "1") == "1":
            SLOT_A = [(xs[0][:], b_xs[0]), (xs[1][:], b_xs[1]), (lnA_bf, b_lnA), (lnB_bf, b_lnB)]
            SLOT_H = [(xs[0][:], b_xs[0]), (xs[1][:], b_xs[1]), (sig_bf[0], b_sig[0]), (sig_bf[1], b_sig[1])]
        else:
            SLOT_A = [(xs[0][:], b_xs[0]), (xs[1][:], b_xs[1]), (xs[0][:], b_xs[0]), (xs[1][:], b_xs[1])]
            SLOT_H = SLOT_A

        def rms_stats(src_ap, src_buf, junk):
            junk_ap, junk_buf = junk
            ci = sscol()
            col = ss[:, ci:ci + 1]
            sc.op("act", lambda h: h.activation(out=junk_ap, in_=src_ap, func=AF.Square, accum_out=col),
                  reads=[src_buf], writes=[junk_buf, b_ss[ci]])
            sc.op("act", lambda h: h.activation(out=col, in_=col, func=AF.Sqrt, scale=1.0 / D, bias=epsr[:]),
                  reads=[b_ss[ci], b_eps], writes=[b_ss[ci]])
            sc.op("dve", lambda h: h.reciprocal(out=col, in_=col), reads=[b_ss[ci]], writes=[b_ss[ci]])
            return ci

        def rms_stats4(srcs, junks):
            while state["ss"] % 4 != 0:
                state["ss"] += 1
            c0 = state["ss"] % 64
            state["ss"] += 4
            bufs4 = [b_ss[c0 + i] for i in range(4)]
            for i in range(4):
                src_ap, src_buf = srcs[i]
                junk_ap, junk_buf = junks[i]
                col = ss[:, c0 + i:c0 + i + 1]
                sc.op("act", lambda h, junk_ap=junk_ap, src_ap=src_ap, col=col: h.activation(
                    out=junk_ap, in_=src_ap, func=AF.Square, accum_out=col),
                    reads=[src_buf], writes=[junk_buf, bufs4[i]])
            c4 = ss[:, c0:c0 + 4]
            sc.op("act", lambda h: h.activation(out=c4, in_=c4, func=AF.Sqrt, scale=1.0 / D, bias=epsr[:]),
                  reads=bufs4 + [b_eps], writes=bufs4)
            sc.op("dve", lambda h: h.reciprocal(out=c4, in_=c4), reads=bufs4, writes=bufs4)
            return c0

        def scaled_copy(t, m, slot):
            gm = 4 * t + m
            sl = gm % NH
            ci = rms_stats(hsl[sl][:], b_hs[sl], slot)
            sc.op("act", lambda h: h.activation(out=slot[0], in_=hsl[sl][:], func=AF.Copy, scale=ss[:, ci:ci + 1]),
                  reads=[b_hs[sl], b_ss[ci]], writes=[slot[1]])

        def transposes(slot, m, dstT, b_dst, gb):
            bk = bank()
            pT = ps[bk][:].bitcast(BF16).rearrange("p (k n) -> p k n", k=KD)
            src_ap, src_buf = slot

            def f(h):
                ins = None
                for kc in range(KD):
                    ins = h.transpose(out=pT[:, kc, :], in_=src_ap[:, kc * 128:(kc + 1) * 128], identity=ident[:])
                return ins
            sc.op("pe", f, reads=[src_buf, b_ident], writes=[b_ps[bk]])
            sc.op("dve", lambda h: h.tensor_tensor(out=dstT[:, :, m * 128:(m + 1) * 128], in0=pT, in1=gb, op=ALU.mult),
                  reads=[b_ps[bk], b_const], writes=[b_dst[m]])

        def mm_z(t, zi):
            g = t * 12 + zi
            s_ = g % NZ
            bk = bank()

            def f(h):
                ins = None
                for kc in range(KD):
                    ins = h.matmul(ps[bk][:], lhsT=zr[s_][:, kc, :], rhs=xnT[:, kc, :],
                                   start=(kc == 0), stop=(kc == KD - 1))
                return ins
            sc.op("pe", f, reads=b_xnT + [b_zr[s_]], writes=[b_ps[bk]])
            load_z(g + NZ)
            return bk

        def stack_u(q):
            for g in range(4):
                bk = bank()

                def f(h, g=g, bk=bk):
                    ins = None
                    for s_ in range(4):
                        ins = h.matmul(ps[bk][32 * s_:32 * s_ + 32, :], lhsT=ident[:, 32 * g:32 * g + 32],
                                       rhs=u[:, q, 31 - 8 * s_:31 - 8 * s_ + TB], start=True, stop=True,
                                       tile_position=(0, 32 * s_))
                    return ins
                sc.op("pe", f, reads=[b_u[q], b_ident], writes=[b_ps[bk]])
                if g % 2 == 0:
                    sc.op("act", lambda h, g=g, bk=bk: h.activation(out=u4[q][:, g, 7:7 + TB], in_=ps[bk][:],
                                                                    func=AF.Copy),
                          reads=[b_ps[bk]], writes=[b_u4[q][g]])
                else:
                    sc.op("dve", lambda h, g=g, bk=bk: h.tensor_copy(out=u4[q][:, g, 7:7 + TB], in_=ps[bk][:]),
                          reads=[b_ps[bk]], writes=[b_u4[q][g]])

        def mixer_gen(t):
            srcs = [(hsl[(4 * t + m) % NH][:], b_hs[(4 * t + m) % NH]) for m in range(4)]
            c0 = rms_stats4(srcs, SLOT_A)
            yield 1
            for m in range(4):
                sc.op("act", lambda h, m=m, c0=c0, src=srcs[m][0]: h.activation(
                    out=SLOT_A[m][0], in_=src, func=AF.Copy, scale=ss[:, c0 + m:c0 + m + 1]),
                    reads=[srcs[m][1], b_ss[c0 + m]], writes=[SLOT_A[m][1]])
            yield 1
            for m in range(4):
                transposes(SLOT_A[m], m, xnT, b_xnT, gmix_b)
                yield (1 if m == 3 else 0)
            zi = 0
            for q in range(4):
                bg = mm_z(t, zi)
                zi += 1
                si = q % 2
                sc.op("act", lambda h, bg=bg, si=si, q=q: h.activation(
                    out=sig[si][:], in_=ps[bg][:], func=AF.Sigmoid, bias=vcol(C_BIN + 4 + q)),
                    reads=[b_ps[bg], b_const], writes=[b_sig[si]])
                yield 0
                ba = mm_z(t, zi)
                zi += 1
                sc.op("dve", lambda h, ba=ba, si=si, q=q: h.scalar_tensor_tensor(
                    out=u[:, q, 31:31 + TB], in0=ps[ba][:], scalar=vcol(C_BIN + q), in1=sig[si][:],
                    op0=ALU.add, op1=ALU.mult),
                    reads=[b_ps[ba], b_sig[si], b_const], writes=[b_u[q]])
                if q >= 1:
                    stack_u(q - 1)
                yield (1 if q % 2 == 1 else 0)
            for i in range(4):
                bp = mm_z(t, zi)
                zi += 1
                if i == 0:
                    stack_u(3)
                sc.op("act", lambda h, bp=bp, i=i: h.activation(
                    out=pb[:, i, 16:16 + TB], in_=ps[bp][:], func=AF.Identity, bias=vcol(C_BIN + 8 + i)),
                    reads=[b_ps[bp], b_const], writes=[b_pb[i]])
                w = 2 ** (i + 1)
                src = pb[:, i, :]
                srcb = b_pb[i]
                lo = 0
                for k in range(i + 1):
                    sh = 2 ** k
                    dst = ptmp[k % 2][:]
                    dstb = b_ptmp[k % 2]
                    lo2 = lo + sh
                    sc.op("dve", lambda h, dst=dst, src=src, lo2=lo2, sh=sh: h.tensor_tensor(
                        out=dst[:, lo2:528], in0=src[:, lo2:528], in1=src[:, lo2 - sh:528 - sh], op=ALU.add),
                        reads=[srcb], writes=[dstb])
                    src, srcb, lo = dst, dstb, lo2
                sc.op("dve", lambda h, src=src, i=i, w=w: h.scalar_tensor_tensor(
                    out=ypool[:, i, :], in0=src[:, 16:16 + TB], scalar=1.0 / w, in1=pb[:, i, 16:16 + TB],
                    op0=ALU.mult, op1=ALU.subtract),
                    reads=[srcb, b_pb[i]], writes=[b_ypool[i]])
                if t == 0:
                    n = w - 1
                    sc.op("dve", lambda h, src=src, n=n: h.tensor_tensor(
                        out=pcorr[:, 0:n], in0=src[:, 16:16 + n], in1=invc[:, 0:n], op=ALU.mult),
                        reads=[srcb, b_invc], writes=[b_pcorr])
                    sc.op("dve", lambda h, i=i, n=n: h.tensor_tensor(
                        out=ypool[:, i, 0:n], in0=pcorr[:, 0:n], in1=pb[:, i, 16:16 + n], op=ALU.subtract),
                        reads=[b_pcorr, b_pb[i]], writes=[b_ypool[i]])
                sc.op("dve", lambda h, i=i: h.tensor_copy(out=pb[:, i, 0:16], in_=pb[:, i, TB:TB + 16]),
                      reads=[b_pb[i]], writes=[b_pb[i]])
                yield (1 if i == 3 else 0)
            bS1 = bank()
            held.add(bS1)
            bS2 = bank()
            held.add(bS2)
            for q in range(4):
                bc = bank()

                def fconv(h, q=q, bc=bc):
                    ins = None
                    for r in range(8):
                        for g in range(4):
                            ins = h.matmul(ps[bc][32 * g:32 * g + 32, :], lhsT=wp[:, q, r, g, :],
                                           rhs=u4[q][:, g, 7 - r:7 - r + TB],
                                           start=(r == 0), stop=(r == 7), tile_position=(0, 32 * g))
                    return ins
                sc.op("pe", fconv, reads=b_u4[q] + [b_wp, b_wpB], writes=[b_ps[bc]])
                sc.op("dve", lambda h, q=q: h.tensor_copy(out=u4[q][:, :, 0:7], in_=u4[q][:, :, TB:TB + 7]),
                      reads=b_u4[q], writes=b_u4[q])
                sc.op("dve", lambda h, q=q: h.tensor_copy(out=u[:, q, 0:31], in_=u[:, q, TB:TB + 31]),
                      reads=[b_u[q]], writes=[b_u[q]])
                vi = q % 2
                sc.op("act", lambda h, q=q, bc=bc: h.activation(
                    out=v[:, q, :], in_=ps[bc][:], func=AF.Identity, bias=vcol(C_BDW + q)),
                    reads=[b_ps[bc], b_const], writes=[b_v[q]])
                sc.op("act", lambda h, q=q, bc=bc, vi=vi: h.activation(
                    out=vbf[vi][:], in_=ps[bc][:], func=AF.Identity, bias=vcol(C_BDW + q)),
                    reads=[b_ps[bc], b_const], writes=[b_vbf[vi]])
                sc.op("act", lambda h, q=q, bc=bc, vi=vi: h.activation(
                    out=sqb[vi][:], in_=ps[bc][:], func=AF.Square, bias=vcol(C_BDW + q)),
                    reads=[b_ps[bc], b_const], writes=[b_sqb[vi]])
                yield 1
                sc.op("pe", lambda h, q=q, vi=vi: h.matmul(ps[bS1][:], lhsT=ones[:], rhs=vbf[vi][:],
                                                           start=(q == 0), stop=(q == 3)),
                      reads=[b_vbf[vi], b_ones], writes=[b_ps[bS1]])
                sc.op("pe", lambda h, q=q, vi=vi: h.matmul(ps[bS2][:], lhsT=ones[:], rhs=sqb[vi][:],
                                                           start=(q == 0), stop=(q == 3)),
                      reads=[b_sqb[vi], b_ones], writes=[b_ps[bS2]])
                if q == 1:
                    for i in range(4):
                        bq = bank()
                        sc.op("pe", lambda h, i=i, bq=bq: h.matmul(ps[bq][:], lhsT=w_pool_sb[:, i, :],
                                                                   rhs=ypool[:, i, :], start=True, stop=True),
                              reads=[b_ypool[i], b_wpool], writes=[b_ps[bq]])
                        sc.op("act", lambda h, i=i, bq=bq: h.activation(
                            out=ysb[:, 4 + i, :], in_=ps[bq][:], func=AF.Copy, scale=vcol(C_SPOOL + i)),
                            reads=[b_ps[bq], b_const], writes=[b_y[4 + i]])
                yield 0
            sc.op("act", lambda h: h.activation(out=lnA[:], in_=ps[bS1][:], func=AF.Square, scale=1.0 / 512),
                  reads=[b_ps[bS1]], writes=[b_lnA])
            sc.op("dve", lambda h: h.scalar_tensor_tensor(out=lnA[:], in0=ps[bS2][:], scalar=1.0 / 512, in1=lnA[:],
                                                          op0=ALU.mult, op1=ALU.subtract),
                  reads=[b_ps[bS2], b_lnA], writes=[b_lnA])
            sc.op("act", lambda h: h.activation(out=lnA[:], in_=lnA[:], func=AF.Sqrt, bias=epsl[:]),
                  reads=[b_lnA, b_eps], writes=[b_lnA])
            sc.op("dve", lambda h: h.reciprocal(out=lnA[:], in_=lnA[:]), reads=[b_lnA], writes=[b_lnA])
            sc.op("dve", lambda h: h.scalar_tensor_tensor(out=lnB[:], in0=ps[bS1][:], scalar=-1.0 / 512, in1=lnA[:],
                                                          op0=ALU.mult, op1=ALU.mult),
                  reads=[b_ps[bS1], b_lnA], writes=[b_lnB])
            held.discard(bS1)
            held.discard(bS2)
            for q in range(4):
                sc.op("dve", lambda h, q=q: h.tensor_tensor(out=v[:, q, :], in0=v[:, q, :], in1=lnA[:], op=ALU.mult),
                      reads=[b_v[q], b_lnA], writes=[b_v[q]])
                sc.op("dve", lambda h, q=q: h.tensor_tensor(out=v[:, q, :], in0=v[:, q, :], in1=lnB[:], op=ALU.add),
                      reads=[b_v[q], b_lnB], writes=[b_v[q]])
                sc.op("act", lambda h, q=q: h.activation(out=ysb[:, q, :], in_=v[:, q, :], func=AF.Silu,
                                                         scale=vcol(C_LNG + q), bias=vcol(C_LNB + q)),
                      reads=[b_v[q], b_const], writes=[b_y[q]])
            yield 5
            for m in range(4):
                gm = 4 * t + m
                sl = gm % NH
                for n in range(2):
                    bk = bank()

                    def f(h, m=m, n=n, bk=bk):
                        ins = None
                        for kc in range(KD):
                            ins = h.matmul(ps[bk][:], lhsT=ysb[:, kc, m * 128:(m + 1) * 128],
                                           rhs=w_out_sb[:, kc, n * TB:(n + 1) * TB],
                                           start=(kc == 0), stop=(kc == KD - 1))
                        return ins
                    sc.op("pe", f, reads=b_y + [b_wout], writes=[b_ps[bk]])
                    sc.op("dve", lambda h, sl=sl, n=n, bk=bk: h.tensor_tensor(
                        out=hsl[sl][:, n * TB:(n + 1) * TB], in0=ps[bk][:], in1=hsl[sl][:, n * TB:(n + 1) * TB],
                        op=ALU.add),
                        reads=[b_ps[bk], b_hs[sl]], writes=[b_hs[sl]])
                    yield 0
                scaled_copy(t, m, SLOT_H[m])
            yield 2

        def hn_gen(t):
            for m in range(4):
                transposes(SLOT_H[m], m, hnT, b_hnT, gffn_b)
                yield 0

        def ffn_up_gen(t):
            for c in range(NF):
                g = t * NF + c
                s_ = g % NR
                bg = bank()
                bu = bank()

                def fg(h, s_=s_, bg=bg):
                    ins = None
                    for kc in range(KD):
                        ins = h.matmul(ps[bg][:], lhsT=gur[s_][:, 0, kc, :], rhs=hnT[:, kc, :],
                                       start=(kc == 0), stop=(kc == KD - 1))
                    return ins

                def fu(h, s_=s_, bu=bu):
                    ins = None
                    for kc in range(KD):
                        ins = h.matmul(ps[bu][:], lhsT=gur[s_][:, 1, kc, :], rhs=hnT[:, kc, :],
                                       start=(kc == 0), stop=(kc == KD - 1))
                    return ins
                sc.op("pe", fg, reads=b_hnT + [b_gur[s_]], writes=[b_ps[bg]])
                sc.op("pe", fu, reads=b_hnT + [b_guru[s_]], writes=[b_ps[bu]])
                si = c % 2
                sc.op("act", lambda h, bg=bg, si=si: h.activation(out=sgf[si][:], in_=ps[bg][:], func=AF.Silu),
                      reads=[b_ps[bg]], writes=[b_sgf[si]])
                sc.op("dve", lambda h, bu=bu, si=si, c=c: h.tensor_tensor(
                    out=hid[:, c, :], in0=ps[bu][:], in1=sgf[si][:], op=ALU.mult),
                    reads=[b_ps[bu], b_sgf[si]], writes=[b_hid[c]])
                load_gu(g + NR)
                yield 1

        def ffn_down(t):
            for c in range(NF):
                g = t * NF + c
                s_ = g % ND

                def f(h, c=c, s_=s_):
                    ins = None
                    for m in range(4):
                        for n in range(2):
                            ins = h.matmul(ps[m * 2 + n][:], lhsT=hid[:, c, m * 128:(m + 1) * 128],
                                           rhs=wdr[s_][:, n * TB:(n + 1) * TB],
                                           start=(c == 0), stop=(c == NF - 1))
                    return ins
                sc.op("pe", f, reads=[b_hid[c], b_wdr[s_]], writes=b_ps)
                load_wd(g + ND)
            for m in range(4):
                gm = 4 * t + m
                sl = gm % NH
                for n in range(2):
                    bk = m * 2 + n
                    sc.op("dve", lambda h, sl=sl, n=n, bk=bk: h.tensor_tensor(
                        out=hsl[sl][:, n * TB:(n + 1) * TB], in0=ps[bk][:], in1=hsl[sl][:, n * TB:(n + 1) * TB],
                        op=ALU.add),
                        reads=[b_ps[bk], b_hs[sl]], writes=[b_hs[sl]])
            state["bank"] = 0

        def final_gen(t):
            yield 1
            srcs = [(hsl[(4 * t + m) % NH][:], b_hs[(4 * t + m) % NH]) for m in range(4)]
            c0 = rms_stats4(srcs, SLOT_H)
            yield 1
            for m in range(4):
                gm = 4 * t + m
                sl = gm % NH
                sc.op("dve", lambda h, sl=sl, ci=c0 + m: h.scalar_tensor_tensor(
                    out=hsl[sl][:], in0=hsl[sl][:], scalar=ss[:, ci:ci + 1], in1=gfin[:],
                    op0=ALU.mult, op1=ALU.mult),
                    reads=[b_hs[sl], b_ss[c0 + m], b_gfin], writes=[b_hs[sl]])
            for m in range(4):
                gm = 4 * t + m
                sl = gm % NH
                sc.dma("sp", y_d[gm * 128:(gm + 1) * 128, :], hsl[sl][:], ds_o[sl], reads=[b_hs[sl]])
            for m in range(4):
                load_x(t + 2, m)
            yield 3

        def run(gen):
            for _ in gen:
                pass

        def chain(*gens):
            for g_ in gens:
                yield from g_

        def merge2(gf, gm):
            for k in gm:
                for _ in range(k):
                    next(gf, None)
            for _ in gf:
                pass

        run(mixer_gen(0))
        run(hn_gen(0))
        for t in range(NTILE):
            gens = []
            if t >= 1:
                gens.append(final_gen(t - 1))
            if t + 1 < NTILE:
                gens.append(mixer_gen(t + 1))
            merge2(ffn_up_gen(t), chain(*gens))
            if t + 1 < NTILE:
                run(hn_gen(t + 1))
            ffn_down(t)
        run(final_gen(NTILE - 1))
        for i in range(NH):
            sc.wait_dma("sp", ds_o[i])

        with nc.Block() as block:
            @block.tensor
            def _(h):
                sc.emit("pe", h)

            @block.scalar
            def _(h):
                sc.emit("act", h)

            @block.vector
            def _(h):
                sc.emit("dve", h)

            @block.gpsimd
            def _(h):
                sc.emit("pool", h)

            @block.sync
            def _(h):
                sc.emit("sp", h)
    nc._sched = sc
    return nc


def _wtab(w_dw):
    wpad = np.concatenate([w_dw, np.zeros((1, 512), np.float32)], axis=0)
    out = np.zeros((128, 4, 8, 4), np.float32)
    for s_ in range(4):
        for r in range(8):
            k = 30 - 8 * s_ - r
            row = wpad[k] if k >= 0 else wpad[31]
            out[32 * s_:32 * s_ + 32, :, r, :] = row.reshape(4, 4, 32).transpose(2, 0, 1)
    return out.reshape(128, 128)


def _prep_shared(inp):
    f = np.float32
    b_in = np.asarray(inp["b_in"], f).reshape(DIN)
    cols = [
        b_in.reshape(12, 128).T,
        np.asarray(inp["g_mix"], f).reshape(8, 128).T,
        np.asarray(inp["g_ffn"], f).reshape(8, 128).T,
        np.asarray(inp["b_dw"], f).reshape(4, 128).T,
        np.asarray(inp["ln_g"], f).reshape(4, 128).T,
        np.asarray(inp["ln_b"], f).reshape(4, 128).T,
        np.asarray(inp["s_pool"], f).reshape(4, 128).T,
        _wtab(np.asarray(inp["w_dw"], f).reshape(CW, 512)),
    ]
    vecs = np.ascontiguousarray(np.concatenate(cols, axis=1), dtype=f)
    assert vecs.shape == (128, NV)
    shared = {
        "w_in": np.ascontiguousarray(np.asarray(inp["w_in"], f).reshape(D, DIN)),
        "w_out": np.ascontiguousarray(np.asarray(inp["w_out"], f).reshape(D, D)),
        "w_pool": np.ascontiguousarray(np.asarray(inp["w_pool"], f).reshape(4, 128, 128)),
        "w_gate": np.ascontiguousarray(np.asarray(inp["w_gate"], f).reshape(D, DFF)),
        "w_up": np.ascontiguousarray(np.asarray(inp["w_up"], f).reshape(D, DFF)),
        "w_down": np.ascontiguousarray(np.asarray(inp["w_down"], f).reshape(DFF, D)),
        "vecs": vecs,
        "gfin": np.ascontiguousarray(np.broadcast_to(np.asarray(inp["g_final"], f).reshape(1, D), (128, D))),
        "ident": np.eye(128, dtype=f),
        "stack32": np.ascontiguousarray(np.tile(np.eye(32, dtype=f), (4, 1))),
        "invc": np.ascontiguousarray(np.broadcast_to((1.0 / np.arange(1, 17, dtype=f)).reshape(1, 16), (128, 16))),
    }
    return shared


_NC_CACHE = {}


def kernel(**inputs):
    x = np.asarray(inputs["x"], np.float32)
    shared = _prep_shared(inputs)
    if "nc" not in _NC_CACHE:
        _NC_CACHE["nc"] = build_nc()
    nc = _NC_CACHE["nc"]
    in_maps = []
    for c in range(NCORES):
        m = dict(shared)
        m["x"] = np.ascontiguousarray(x[c])
        in_maps.append(m)
    res = run_bass_kernel_spmd(nc, in_maps, core_ids=list(range(NCORES)))
    out = np.stack([np.asarray(res.results[c]["y"], np.float32) for c in range(NCORES)], axis=0)
    return out
```
